# Optimizing a Trainium2 kernel written in Bass

```python
import math
import jax, jax.numpy as jnp
from jax import lax
import numpy as np

D_MODEL = 2048
BATCH = 2
SEQ = 4096
DEPTH = 2

D_MIX = D_MODEL
D_POOL = D_MIX // 4
D_HYENA = 3 * D_MIX // 8
D_GMLP = D_MIX - D_POOL - D_HYENA
POOL_WINDOWS = (2, 4, 8, 16)
N_POOL_GROUPS = len(POOL_WINDOWS)
POOL_GROUP_DIM = D_POOL // N_POOL_GROUPS
HYENA_GROUP_DIM = 128
N_HYENA_GROUPS = D_HYENA // HYENA_GROUP_DIM
FILTER_BANDS = 16
FILTER_EMB = 1 + 2 * FILTER_BANDS
FILTER_HIDDEN = 64
DECAY_TARGET = 1e-2
FAST_DECAY_PCT = 0.3
SLOW_DECAY_PCT = 1.5
CHUNK = 128
GMLP_HEAD_DIM = 128
N_GMLP_HEADS = D_GMLP // GMLP_HEAD_DIM
D_IN_PROJ = D_POOL + 3 * D_HYENA + 2 * D_GMLP
D_FF = 5632
CONV_WIDTH = 3
RMS_EPS = 1e-6
LN_EPS = 1e-5

kernel_name = "hybrid_pool_hyena_gmlp_encoder"


def rmsnorm(x, g):
    xf = x.astype(jnp.float32)
    y = xf * lax.rsqrt(jnp.mean(xf * xf, axis=-1, keepdims=True) + RMS_EPS)
    return (y * g.astype(jnp.float32)).astype(x.dtype)


def layernorm(x, g, b):
    xf = x.astype(jnp.float32)
    mu = jnp.mean(xf, axis=-1, keepdims=True)
    var = jnp.mean(jnp.square(xf - mu), axis=-1, keepdims=True)
    y = (xf - mu) * lax.rsqrt(var + LN_EPS)
    return (y * g.astype(jnp.float32) + b.astype(jnp.float32)).astype(x.dtype)


def dwconv3(x, w, b):
    L = x.shape[1]
    xp = jnp.pad(x, ((0, 0), (1, 1), (0, 0)))
    return xp[:, :L] * w[0] + xp[:, 1:L + 1] * w[1] + xp[:, 2:] * w[2] + b


def multiscale_pool_mixer(xa, w_pool, b_pool, scale):
    B, L, _ = xa.shape
    xf = xa.reshape(B, L, N_POOL_GROUPS, POOL_GROUP_DIM).astype(jnp.float32)
    csum = jnp.concatenate([jnp.zeros_like(xf[:, :1]), jnp.cumsum(xf, axis=1)], axis=1)
    t = np.arange(L)
    pooled = []
    for g, w in enumerate(POOL_WINDOWS):
        lo = np.maximum(t - w // 2, 0)
        hi = np.minimum(t + w // 2 - 1, L - 1)
        cnt = (hi - lo + 1).astype(np.float32)
        s = csum[:, hi + 1, g] - csum[:, lo, g]
        pooled.append(s / cnt[None, :, None])
    pooled = jnp.stack(pooled, axis=2)
    diff = (pooled - xf).astype(xa.dtype)
    y = jnp.einsum('blgc,gcd->blgd', diff, w_pool) + b_pool
    return y.reshape(B, L, D_POOL) * scale


def filter_features(L):
    t = jnp.linspace(0.0, 1.0, L, dtype=jnp.float32)[:, None]
    w = 2.0 * math.pi * jnp.arange(L, dtype=jnp.float32)[:, None] / L
    f = jnp.linspace(1e-4, FILTER_BANDS - 1, FILTER_BANDS, dtype=jnp.float32)[None, :]
    return jnp.concatenate([t, jnp.cos(f * w), -jnp.sin(f * w)], axis=-1), t


def implicit_filters(feat, t, w1, b1, fr1, w2, b2, fr2, w3):
    f32 = lambda a: a.astype(jnp.float32)
    h = jnp.sin(f32(fr1) * (feat @ f32(w1) + f32(b1)))
    h = jnp.sin(f32(fr2) * (h @ f32(w2) + f32(b2)))
    h = h @ f32(w3)
    deltas = jnp.linspace(math.log(DECAY_TARGET) / SLOW_DECAY_PCT,
                          math.log(DECAY_TARGET) / FAST_DECAY_PCT, D_HYENA, dtype=jnp.float32)
    decay = jnp.exp(-t * jnp.abs(deltas)[None, :])
    return h[:, :D_HYENA] * decay, h[:, D_HYENA:] * decay


def bidirectional_fftconv(z, h_fwd, h_bwd):
    B, L, C = z.shape
    k = jnp.concatenate([h_fwd, jnp.zeros((1, C), h_fwd.dtype), h_bwd[:0:-1]], axis=0)
    k_f = jnp.fft.rfft(k, n=2 * L, axis=0)
    z_f = jnp.fft.rfft(z.astype(jnp.float32), n=2 * L, axis=1)
    return jnp.fft.irfft(z_f * k_f[None], n=2 * L, axis=1)[:, :L]


def hyena_mixer(xb, short_w, short_b, h_fwd, h_bwd, d_skip):
    xs = dwconv3(xb, short_w, short_b)
    x0, x1, v = jnp.split(xs, 3, axis=-1)
    z = x1 * v
    y = bidirectional_fftconv(z, h_fwd, h_bwd).astype(z.dtype) + d_skip * z
    return x0 * y


def chunked_spatial_gating(xc, ln_g, ln_b, w_s, b_s):
    B, L, _ = xc.shape
    z = jax.nn.gelu(xc)
    u, v = jnp.split(z, 2, axis=-1)
    v = layernorm(v, ln_g, ln_b).reshape(B, L // CHUNK, CHUNK, N_GMLP_HEADS, GMLP_HEAD_DIM)
    g = jnp.einsum('bnpec,eqp->bnqec', v, w_s) + b_s.T[:, :, None]
    return u * g.reshape(B, L, D_GMLP)


def setup_inputs(seed: int = 0) -> dict:
    key = jax.random.key(seed)
    ks = jax.random.split(key, 27)
    nrm = lambda k, shape, s: s * jax.random.normal(k, shape, jnp.float32)
    c = POOL_GROUP_DIM
    return {
        "x": nrm(ks[0], (BATCH, SEQ, D_MODEL), 1.0),
        "norm_mix": 1.0 + nrm(ks[1], (DEPTH, D_MODEL), 0.05),
        "w_in": nrm(ks[2], (DEPTH, D_MODEL, D_IN_PROJ), D_MODEL ** -0.5),
        "pool_w": nrm(ks[3], (DEPTH, N_POOL_GROUPS, c, c), c ** -0.5),
        "pool_b": nrm(ks[4], (DEPTH, N_POOL_GROUPS, c), 0.02),
        "pool_scale": 1.0 + nrm(ks[5], (DEPTH, D_POOL), 0.1),
        "hy_short_w": nrm(ks[6], (DEPTH, CONV_WIDTH, 3 * D_HYENA), 0.5),
        "hy_short_b": nrm(ks[7], (DEPTH, 3 * D_HYENA), 0.02),
        "hy_filt_w1": nrm(ks[8], (DEPTH, FILTER_EMB, FILTER_HIDDEN), FILTER_EMB ** -0.5),
        "hy_filt_b1": nrm(ks[9], (DEPTH, FILTER_HIDDEN), 0.5),
        "hy_filt_freq1": 1.0 + nrm(ks[10], (DEPTH, FILTER_HIDDEN), 0.1),
        "hy_filt_w2": nrm(ks[11], (DEPTH, FILTER_HIDDEN, FILTER_HIDDEN), FILTER_HIDDEN ** -0.5),
        "hy_filt_b2": nrm(ks[12], (DEPTH, FILTER_HIDDEN), 0.5),
        "hy_filt_freq2": 1.0 + nrm(ks[13], (DEPTH, FILTER_HIDDEN), 0.1),
        "hy_filt_w3": nrm(ks[14], (DEPTH, FILTER_HIDDEN, 2 * D_HYENA), 0.005),
        "hy_skip": nrm(ks[15], (DEPTH, D_HYENA), 0.5),
        "gm_ln_g": 1.0 + nrm(ks[16], (DEPTH, D_GMLP), 0.05),
        "gm_ln_b": nrm(ks[17], (DEPTH, D_GMLP), 0.02),
        "gm_w_s": nrm(ks[18], (DEPTH, N_GMLP_HEADS, CHUNK, CHUNK), CHUNK ** -0.5),
        "gm_b_s": 1.0 + nrm(ks[19], (DEPTH, N_GMLP_HEADS, CHUNK), 0.1),
        "w_out": nrm(ks[20], (DEPTH, D_MIX, D_MODEL), D_MIX ** -0.5),
        "norm_ffn": 1.0 + nrm(ks[21], (DEPTH, D_MODEL), 0.05),
        "ffn_w_up": nrm(ks[22], (DEPTH, D_MODEL, 2 * D_FF), D_MODEL ** -0.5),
        "ffn_conv_w": nrm(ks[23], (DEPTH, CONV_WIDTH, 2 * D_FF), 0.5),
        "ffn_conv_b": nrm(ks[24], (DEPTH, 2 * D_FF), 0.02),
        "ffn_w_down": nrm(ks[25], (DEPTH, D_FF, D_MODEL), D_FF ** -0.5),
        "norm_final": 1.0 + nrm(ks[26], (D_MODEL,), 0.05),
    }


def reference(x, norm_mix, w_in, pool_w, pool_b, pool_scale, hy_short_w, hy_short_b,
              hy_filt_w1, hy_filt_b1, hy_filt_freq1, hy_filt_w2, hy_filt_b2, hy_filt_freq2,
              hy_filt_w3, hy_skip, gm_ln_g, gm_ln_b, gm_w_s, gm_b_s, w_out, norm_ffn,
              ffn_w_up, ffn_conv_w, ffn_conv_b, ffn_w_down, norm_final):
    L = x.shape[1]
    feat, t = filter_features(L)
    a_end = D_POOL
    b_end = D_POOL + 3 * D_HYENA
    for i in range(DEPTH):
        h = rmsnorm(x, norm_mix[i])
        p = h @ w_in[i]
        y_a = multiscale_pool_mixer(p[..., :a_end], pool_w[i], pool_b[i], pool_scale[i])
        h_fwd, h_bwd = implicit_filters(feat, t, hy_filt_w1[i], hy_filt_b1[i], hy_filt_freq1[i],
                                        hy_filt_w2[i], hy_filt_b2[i], hy_filt_freq2[i], hy_filt_w3[i])
        y_b = hyena_mixer(p[..., a_end:b_end], hy_short_w[i], hy_short_b[i], h_fwd, h_bwd, hy_skip[i])
        y_c = chunked_spatial_gating(p[..., b_end:], gm_ln_g[i], gm_ln_b[i], gm_w_s[i], gm_b_s[i])
        x = x + jnp.concatenate([y_a, y_b, y_c], axis=-1) @ w_out[i]
        h = rmsnorm(x, norm_ffn[i])
        up = dwconv3(h @ ffn_w_up[i], ffn_conv_w[i], ffn_conv_b[i])
        gate, val = jnp.split(up, 2, axis=-1)
        x = x + (jax.nn.silu(gate) * val) @ ffn_w_down[i]
    return rmsnorm(x, norm_final)
```

```python
import math
from contextlib import ExitStack
import numpy as np
import ml_dtypes
import concourse.bass as bass
import concourse.mybir as mybir
from concourse.bass_utils import run_bass_kernel_spmd

F32 = mybir.dt.float32
BF16 = mybir.dt.bfloat16
AF = mybir.ActivationFunctionType
ALU = mybir.AluOpType
AX = mybir.AxisListType
NPBF = ml_dtypes.bfloat16

D_MODEL = 2048; SEQ = 4096; BATCH = 2; DEPTH = 2
D_POOL = 512; D_HY = 768; D_GM = 768; D_IN = 4352; D_FF = 5632
TOK = 1024
HALO = 8
TH = TOK + 2 * HALO
RMS_EPS = 1e-6; LN_EPS = 1e-5
SAME_ENGINE_SYNC = True
import os
DBG_STOP = int(os.environ.get('DBG_STOP', '99'))


class Buf:
    def __init__(self, name=""):
        self.name = name
        self.w = []
        self.r = []


class Prog:
    ENG = ("pe", "act", "dve", "pool", "sp")

    def __init__(self, nc, es, n_dma_sems=(12, 12)):
        self.nc = nc
        self.es = es
        self.q = {e: [] for e in self.ENG}
        self.sem = {e: es.enter_context(nc.semaphore("c_" + e)) for e in ("pe", "act", "dve", "pool")}
        self.cnt = {e: 0 for e in ("pe", "act", "dve", "pool")}
        self.seen = {e: {} for e in self.ENG}
        self.dma_sems = {}
        for e, n in zip(("pool", "sp"), n_dma_sems):
            self.dma_sems[e] = [[es.enter_context(nc.semaphore(f"d_{e}{i}")), 0, None] for i in range(n)]
        self.dma_i = {"pool": 0, "sp": 0}
        self.semid = {}
        self.final = []
        self.psum = []
        self.psum_i = 0

    def _sid(self, sem):
        return id(sem)

    def op(self, eng, fn, reads=(), writes=(), dma=False, final=False, add_w=False):
        deps = []
        for b in reads:
            deps += b.w
        for b in writes:
            deps += b.w + b.r
        if dma:
            slot = self.dma_sems[eng][self.dma_i[eng] % len(self.dma_sems[eng])]
            self.dma_i[eng] += 1
            if slot[2] is not None:
                deps.append(slot[2])
            slot[1] += 16
            tok = (slot[0], slot[1], "dma")
            slot[2] = tok
            inc = 16
        else:
            self.cnt[eng] += 1
            tok = (self.sem[eng], self.cnt[eng], eng)
            inc = 1
        waits = {}
        for (s, v, src) in deps:
            if src == eng and not dma and not SAME_ENGINE_SYNC:
                continue
            k = self._sid(s)
            if self.seen[eng].get(k, 0) >= v:
                continue
            if k not in waits or waits[k][1] < v:
                waits[k] = (s, v)
        for k, (s, v) in waits.items():
            self.seen[eng][k] = v
        self.q[eng].append((list(waits.values()), fn, tok[0], inc))
        for b in reads:
            b.r.append(tok)
        for b in writes:
            if add_w:
                b.w.append(tok)
            else:
                b.w = [tok]
                b.r = []
        if final:
            self.final.append(tok)
        return tok

    def emit(self):
        nc = self.nc
        fin = {}
        for (s, v, _) in self.final:
            k = self._sid(s)
            if k not in fin or fin[k][1] < v:
                fin[k] = (s, v)
        q = self.q
        finals = list(fin.values())

        def run(eng_obj, name):
            for waits, fn, sem, inc in q[name]:
                for (s, v) in waits:
                    eng_obj.wait_ge(s, v)
                ins = fn(eng_obj)
                ins.then_inc(sem, inc)
            if name == "sp":
                for (s, v) in finals:
                    eng_obj.wait_ge(s, v)

        with nc.Block() as block:
            @block.tensor
            def _(e):
                run(e, "pe")

            @block.scalar
            def _(e):
                run(e, "act")

            @block.vector
            def _(e):
                run(e, "dve")

            @block.gpsimd
            def _(e):
                run(e, "pool")

            @block.sync
            def _(e):
                run(e, "sp")

    def tile(self, name, shape, dt):
        return self.es.enter_context(self.nc.sbuf_tensor("sb_" + name, list(shape), dt))

    def alloc_psum(self, n=8):
        for i in range(n):
            t = self.es.enter_context(self.nc.psum_tensor(f"ps{i}", [128, 512], F32))
            self.psum.append((t, Buf(f"ps{i}")))

    def next_psum(self):
        t, b = self.psum[self.psum_i % len(self.psum)]
        self.psum_i += 1
        return t, b

    def dma(self, eng, out, in_, reads=(), writes=(), final=False, add_w=False):
        return self.op(eng, lambda e: e.dma_start(out=out, in_=in_), reads=reads, writes=writes, dma=True,
                       final=final, add_w=add_w)

    def mm(self, out, pairs, reads, wbuf):
        def fn(pe):
            n = len(pairs)
            for i, (l, r) in enumerate(pairs):
                ins = pe.matmul(out, l, r, start=(i == 0), stop=(i == n - 1))
            return ins
        return self.op("pe", fn, reads=reads, writes=[wbuf])


def pieces(n, maxw=512):
    k = (n + maxw - 1) // maxw
    base = (n + k - 1) // k
    out = []
    s = 0
    while s < n:
        w = min(base, n - s)
        out.append((s, w))
        s += w
    return out


class WStream:
    def __init__(self, P, name, kc_max, ncol_max, nslots):
        self.P = P
        self.slots = [(P.tile(f"{name}{i}", [128, kc_max, ncol_max], BF16), Buf(f"{name}{i}")) for i in range(nslots)]
        self.i = 0

    def load(self, w_ap, r0, kc, c0, ncol):
        t, b = self.slots[self.i % len(self.slots)]
        self.i += 1
        src = w_ap[r0:r0 + kc * 128, c0:c0 + ncol].rearrange("(k p) n -> p k n", p=128)
        self.P.dma("pool", t[:, 0:kc, 0:ncol], src, writes=[b])
        return t, b


def proj(P, ws, w_ap, r0, kc, c0, in_aps, in_bufs, ncols, consume, pcs=None):
    wt, wb = ws.load(w_ap, r0, kc, c0, 128)
    pcs = pcs or pieces(ncols)
    for pi, (s, w) in enumerate(pcs):
        pt, pb = P.next_psum()
        pairs = [(wt[:, k, 0:128], in_aps[k][:, s:s + w]) for k in range(kc)]
        P.mm(pt[:, 0:w], pairs, reads=[wb] + list(in_bufs), wbuf=pb)
        consume(pi, s, w, pt, pb)


def build_A():
    nc = bass.Bass("TRN2", target_bir_lowering=False)

    def din(name, shape, dt=F32):
        return nc.dram_tensor(name, list(shape), dt, kind="ExternalInput").ap()

    xh = din("xh", [128, 16, TH])
    g_mix = din("g_mix", [128, 16])
    w_in = din("w_in", [D_MODEL, D_IN])
    pool_w = din("pool_w", [512, 128])
    pool_b = din("pool_b", [128, 4])
    pool_s = din("pool_s", [128, 4])
    rc_edge = din("rc_edge", [128, 4, 16])
    cw = din("cw", [128, 18, 3])
    cb = din("cb", [128, 18])
    ln_g = din("ln_g", [128, 768])
    ln_b = din("ln_b", [128, 768])
    wsT = din("wsT", [768, 128])
    bsb = din("bsb", [128, 768])
    ypart = nc.dram_tensor("ypart", [128, 16, TOK], BF16, kind="ExternalOutput").ap()
    zout = nc.dram_tensor("zout", [128, 6, TOK], BF16, kind="ExternalOutput").ap()

    with ExitStack() as es:
        P = Prog(nc, es)
        P.alloc_psum(8)
        T = P.tile
        hT = T("hT", [128, 16, TH], BF16); hB = [Buf(f"h{k}") for k in range(16)]
        yT = T("yT", [128, 16, TOK], BF16); yB = [Buf(f"y{k}") for k in range(16)]
        zT = T("zT", [128, 6, TOK], BF16); zB = [Buf(f"z{k}") for k in range(6)]
        xs = [(T(f"xs{i}", [128, TH], F32), Buf()) for i in range(3)]
        sqs = [(T(f"sq{i}", [128, TH], BF16), Buf()) for i in range(2)]
        rstd = T("rstd", [128, TH], F32); rstdB = Buf()
        ones = T("ones", [128, 128], BF16); onesB = Buf()
        gm = T("gm", [128, 16], F32); gmB = Buf()
        pb_t = T("pb", [128, 4], F32); ps_t = T("psc", [128, 4], F32); pbs_t = T("pbs", [128, 4], F32)
        smallB = Buf()
        rce = T("rce", [128, 4, 16], F32)
        cw_t = T("cw", [128, 18, 3], F32); cb_t = T("cb", [128, 18], F32)
        lng = T("lng", [128, 768], F32); lnb = T("lnb", [128, 768], F32); bs_t = T("bs", [128, 768], F32)
        wpool = T("wpool", [128, 4, 128], BF16); wpoolB = Buf()
        wst = T("wst", [128, 6, 128], BF16); wstB = Buf()
        wv = T("wv", [128, 16, 768], BF16); wvB = Buf()
        ws = WStream(P, "ws", 16, 128, 4)
        pbuf = [(T(f"pbuf{i}", [128, TH], F32), Buf()) for i in range(2)]
        sa = T("sa", [128, TH], F32); sb = T("sb", [128, TH], F32); saB = Buf(); sbB = Buf()
        diff = T("diff", [128, TOK], BF16); diffB = Buf()
        x1s = T("x1s", [128, TOK], F32); x1sB = Buf()
        cv = T("cv", [128, TOK], F32); cvB = Buf()
        g1 = T("g1", [128, TH], F32); g2 = T("g2", [128, TH], F32); g1B = Buf(); g2B = Buf()
        vt = T("vt", [128, 768], F32); vtB = Buf()
        v1 = T("v1", [128, 768], F32); v1B = Buf()
        v2 = T("v2", [128, 768], F32); v2B = Buf()
        vn = T("vn", [128, 768], BF16); vnB = Buf()
        st6 = T("st6", [128, 3, 6], F32); mv = T("mv", [128, 2], F32); stB = Buf()
        rs_ln = T("rsln", [128, 1], F32)

        P.op("dve", lambda e: e.memset(ones[:, :], 1.0), writes=[onesB])
        P.dma("sp", gm[:, :], g_mix, writes=[gmB])
        for (t_, d_) in ((pb_t, pool_b), (ps_t, pool_s)):
            P.dma("sp", t_[:, :], d_, writes=[smallB], add_w=True)
        P.dma("sp", rce[:, :, :], rc_edge, writes=[smallB], add_w=True)
        P.dma("sp", cw_t[:, :, :], cw, writes=[smallB], add_w=True)
        P.dma("sp", cb_t[:, :], cb, writes=[smallB], add_w=True)
        P.dma("sp", lng[:, :], ln_g, writes=[smallB], add_w=True)
        P.dma("sp", lnb[:, :], ln_b, writes=[smallB], add_w=True)
        P.dma("sp", bs_t[:, :], bsb, writes=[smallB], add_w=True)
        P.dma("pool", wpool[:, :, :], pool_w.rearrange("(g c) d -> c g d", c=128), writes=[wpoolB])
        P.dma("pool", wst[:, :, :], wsT.rearrange("(e p) q -> p e q", p=128), writes=[wstB])
        P.op("dve", lambda e: e.tensor_tensor(out=pbs_t[:, :], in0=pb_t[:, :], in1=ps_t[:, :], op=ALU.mult),
             reads=[smallB], writes=[smallB], add_w=True)

        pcs_h = pieces(TH)
        slots = [P.next_psum() for _ in pcs_h]
        for k in range(16):
            xt, xb = xs[k % 3]
            P.dma("sp", xt[:, :], xh[:, k, :], writes=[xb])
            sq, sqb = sqs[k % 2]
            P.op("act", lambda e, xt=xt, sq=sq: e.activation(out=sq[:, :], in_=xt[:, :], func=AF.Square),
                 reads=[xb], writes=[sqb])
            for (s, w), (pt, pb) in zip(pcs_h, slots):
                def fn(pe, k=k, s=s, w=w, pt=pt, sq=sq):
                    return pe.matmul(pt[:, 0:w], ones[:, :], sq[:, s:s + w], start=(k == 0), stop=(k == 15))
                if k == 0:
                    P.op("pe", fn, reads=[sqb, onesB], writes=[pb])
                else:
                    tok = P.op("pe", fn, reads=[sqb, onesB, pb])
                    pb.w = [tok]; pb.r = []
        for (s, w), (pt, pb) in zip(pcs_h, slots):
            P.op("dve", lambda e, s=s, w=w, pt=pt: e.tensor_scalar(
                out=rstd[:, s:s + w], in0=pt[:, 0:w], scalar1=1.0 / D_MODEL, scalar2=RMS_EPS,
                op0=ALU.mult, op1=ALU.add), reads=[pb], writes=[rstdB])
        P.op("act", lambda e: e.activation(out=rstd[:, :], in_=rstd[:, :], func=AF.Sqrt), reads=[rstdB], writes=[rstdB])
        P.op("dve", lambda e: e.reciprocal(out=rstd[:, :], in_=rstd[:, :]), reads=[rstdB], writes=[rstdB])
        for k in range(16):
            xt, xb = xs[k % 3]
            P.dma("sp", xt[:, :], xh[:, k, :], writes=[xb])
            P.op("dve", lambda e, k=k, xt=xt: e.scalar_tensor_tensor(
                out=hT[:, k, :], in0=xt[:, :], scalar=gm[:, k:k + 1], in1=rstd[:, :], op0=ALU.mult, op1=ALU.mult),
                reads=[xb, gmB, rstdB], writes=[hB[k]])

        h_aps = [hT[:, k, :] for k in range(16)]
        ev_i = [0]

        def evac_to(dst_t, dst_b):
            def consume(pi, s, w, pt, pb):
                P.op("act", lambda e: e.activation(out=dst_t[:, s:s + w], in_=pt[:, 0:w], func=AF.Identity),
                     reads=[pb], writes=[dst_b])
            return consume

        for g, win in enumerate((2, 4, 8, 16) if DBG_STOP >= 2 else ()):
            pt_, pB = pbuf[ev_i[0] % 2]; ev_i[0] += 1
            proj(P, ws, w_in, 0, 16, g * 128, h_aps, hB, TH, evac_to(pt_, pB))
            P.op("dve", lambda e, p=pt_: e.tensor_tensor(out=sa[:, 1:TH], in0=p[:, 0:TH - 1], in1=p[:, 1:TH], op=ALU.add),
                 reads=[pB], writes=[saB])
            cur, curB, oth, othB = sa, saB, sb, sbB
            lo, hi = 1, TH
            sh = 1
            for _ in range(g):
                nlo, nhi = lo + sh, hi - sh
                P.op("dve", lambda e, cur=cur, oth=oth, nlo=nlo, nhi=nhi, sh=sh: e.tensor_tensor(
                    out=oth[:, nlo:nhi], in0=cur[:, nlo - sh:nhi - sh], in1=cur[:, nlo + sh:nhi + sh], op=ALU.add),
                    reads=[curB], writes=[othB])
                cur, curB, oth, othB = oth, othB, cur, curB
                lo, hi = nlo, nhi
                sh *= 2
            P.op("dve", lambda e, cur=cur, p=pt_, win=win: e.scalar_tensor_tensor(
                out=diff[:, :], in0=cur[:, HALO:HALO + TOK], scalar=1.0 / win, in1=p[:, HALO:HALO + TOK],
                op0=ALU.mult, op1=ALU.subtract), reads=[curB, pB], writes=[diffB])
            for (c0, e0) in ((0, 0), (TOK - 8, 8)):
                P.op("dve", lambda e, cur=cur, c0=c0, e0=e0, g=g: e.tensor_tensor(
                    out=oth[:, 0:8], in0=cur[:, HALO + c0:HALO + c0 + 8], in1=rce[:, g, e0:e0 + 8], op=ALU.mult),
                    reads=[curB, smallB], writes=[othB])
                P.op("dve", lambda e, c0=c0, p=pt_: e.tensor_tensor(
                    out=diff[:, c0:c0 + 8], in0=oth[:, 0:8], in1=p[:, HALO + c0:HALO + c0 + 8], op=ALU.subtract),
                    reads=[othB, pB], writes=[diffB])
            for (s, w) in pieces(TOK):
                pt, pb = P.next_psum()
                P.mm(pt[:, 0:w], [(wpool[:, g, :], diff[:, s:s + w])], reads=[wpoolB, diffB], wbuf=pb)
                P.op("act", lambda e, g=g, s=s, w=w, pt=pt: e.activation(
                    out=yT[:, g, s:s + w], in_=pt[:, 0:w], func=AF.Identity, bias=pbs_t[:, g:g + 1],
                    scale=ps_t[:, g:g + 1]), reads=[pb, smallB], writes=[yB[g]])

        def conv3(src, srcB, ch, dst_ap, dstB, extra_reads=()):
            P.op("dve", lambda e: e.tensor_scalar(
                out=cv[:, :], in0=src[:, HALO:HALO + TOK], scalar1=cw_t[:, ch, 1:2], scalar2=cb_t[:, ch:ch + 1],
                op0=ALU.mult, op1=ALU.add), reads=[srcB, smallB], writes=[cvB])
            P.op("dve", lambda e: e.scalar_tensor_tensor(
                out=cv[:, :], in0=src[:, HALO - 1:HALO - 1 + TOK], scalar=cw_t[:, ch, 0:1], in1=cv[:, :],
                op0=ALU.mult, op1=ALU.add), reads=[srcB, cvB, smallB], writes=[cvB])
            return dst_ap

        for e_ in range(6 if DBG_STOP >= 3 else 0):
            ch = e_
            pt_, pB = pbuf[ev_i[0] % 2]; ev_i[0] += 1
            proj(P, ws, w_in, 0, 16, D_POOL + ch * 128, h_aps, hB, TH, evac_to(pt_, pB))
            conv3(pt_, pB, ch, None, None)
            P.op("dve", lambda e, p=pt_, ch=ch, e_=e_: e.scalar_tensor_tensor(
                out=yT[:, 4 + e_, :], in0=p[:, HALO + 1:HALO + 1 + TOK], scalar=cw_t[:, ch, 2:3], in1=cv[:, :],
                op0=ALU.mult, op1=ALU.add), reads=[pB, cvB, smallB], writes=[yB[4 + e_]])
            ch = 6 + e_
            pt_, pB = pbuf[ev_i[0] % 2]; ev_i[0] += 1
            proj(P, ws, w_in, 0, 16, D_POOL + ch * 128, h_aps, hB, TH, evac_to(pt_, pB))
            conv3(pt_, pB, ch, None, None)
            P.op("dve", lambda e, p=pt_, ch=ch: e.scalar_tensor_tensor(
                out=x1s[:, :], in0=p[:, HALO + 1:HALO + 1 + TOK], scalar=cw_t[:, ch, 2:3], in1=cv[:, :],
                op0=ALU.mult, op1=ALU.add), reads=[pB, cvB, smallB], writes=[x1sB])
            ch = 12 + e_
            pt_, pB = pbuf[ev_i[0] % 2]; ev_i[0] += 1
            proj(P, ws, w_in, 0, 16, D_POOL + ch * 128, h_aps, hB, TH, evac_to(pt_, pB))
            conv3(pt_, pB, ch, None, None)
            P.op("dve", lambda e, p=pt_, ch=ch: e.scalar_tensor_tensor(
                out=cv[:, :], in0=p[:, HALO + 1:HALO + 1 + TOK], scalar=cw_t[:, ch, 2:3], in1=cv[:, :],
                op0=ALU.mult, op1=ALU.add), reads=[pB, cvB, smallB], writes=[cvB])
            P.op("dve", lambda e, e_=e_: e.tensor_tensor(out=zT[:, e_, :], in0=x1s[:, :], in1=cv[:, :], op=ALU.mult),
                 reads=[x1sB, cvB], writes=[zB[e_]])

        def gelu_fm(src, n, out_ap, srcB, outB):
            P.op("act", lambda e: e.activation(out=g1[:, 0:n], in_=src, func=AF.Square), reads=[srcB], writes=[g1B])
            P.op("dve", lambda e: e.tensor_scalar(out=g1[:, 0:n], in0=g1[:, 0:n], scalar1=0.044715, scalar2=1.0,
                                                  op0=ALU.mult, op1=ALU.add), reads=[g1B], writes=[g1B])
            P.op("dve", lambda e: e.tensor_tensor(out=g1[:, 0:n], in0=g1[:, 0:n], in1=src, op=ALU.mult),
                 reads=[g1B, srcB], writes=[g1B])
            P.op("act", lambda e: e.activation(out=g2[:, 0:n], in_=g1[:, 0:n], func=AF.Sigmoid,
                                               scale=1.5957691216057308), reads=[g1B], writes=[g2B])
            P.op("dve", lambda e: e.tensor_tensor(out=out_ap, in0=g2[:, 0:n], in1=src, op=ALU.mult),
                 reads=[g2B, srcB], writes=[outB])

        UC0 = D_POOL + 3 * D_HY
        for e_ in range(6 if DBG_STOP >= 4 else 0):
            pt_, pB = pbuf[ev_i[0] % 2]; ev_i[0] += 1
            proj(P, ws, w_in, 0, 16, UC0 + e_ * 128, h_aps, hB, TH, evac_to(pt_, pB))
            gelu_fm(pt_[:, HALO:HALO + TOK], TOK, yT[:, 10 + e_, :], pB, yB[10 + e_])

        VC0 = UC0 + D_GM
        for half in range(2):
            P.dma("pool", wv[:, half * 8:(half + 1) * 8, :],
                  w_in[half * 1024:(half + 1) * 1024, VC0:VC0 + 768].rearrange("(k p) n -> p k n", p=128),
                  writes=[wvB], add_w=True)
        for blk in range(8 if DBG_STOP >= 5 else 0):
            t0 = HALO + blk * 128
            for (c0, cw_) in ((0, 512), (512, 256)):
                pt, pb = P.next_psum()
                pairs = [(hT[:, k, t0:t0 + 128], wv[:, k, c0:c0 + cw_]) for k in range(16)]
                P.mm(pt[:, 0:cw_], pairs, reads=[wvB] + hB, wbuf=pb)
                P.op("act", lambda e, pt=pt, c0=c0, cw_=cw_: e.activation(out=vt[:, c0:c0 + cw_], in_=pt[:, 0:cw_],
                                                                         func=AF.Identity), reads=[pb], writes=[vtB])
            if DBG_STOP < 6:
                continue
            P.op("act", lambda e: e.activation(out=v1[:, :], in_=vt[:, :], func=AF.Square), reads=[vtB], writes=[v1B])
            P.op("dve", lambda e: e.tensor_scalar(out=v1[:, :], in0=v1[:, :], scalar1=0.044715, scalar2=1.0,
                                                  op0=ALU.mult, op1=ALU.add), reads=[v1B], writes=[v1B])
            P.op("dve", lambda e: e.tensor_tensor(out=v1[:, :], in0=v1[:, :], in1=vt[:, :], op=ALU.mult),
                 reads=[v1B, vtB], writes=[v1B])
            P.op("act", lambda e: e.activation(out=v2[:, :], in_=v1[:, :], func=AF.Sigmoid, scale=1.5957691216057308),
                 reads=[v1B], writes=[v2B])
            P.op("dve", lambda e: e.tensor_tensor(out=v2[:, :], in0=v2[:, :], in1=vt[:, :], op=ALU.mult),
                 reads=[v2B, vtB], writes=[v2B])
            if DBG_STOP < 7:
                continue
            for c in range(3):
                P.op("dve", lambda e, c=c: e.bn_stats(out=st6[:, c, :], in_=v2[:, c * 256:(c + 1) * 256]),
                     reads=[v2B], writes=[stB] if c == 0 else [stB])
            P.op("dve", lambda e: e.bn_aggr(out=mv[:, :], in_=st6[:, :, :]), reads=[stB], writes=[stB])
            P.op("dve", lambda e: e.tensor_scalar(out=rs_ln[:, :], in0=mv[:, 1:2], scalar1=LN_EPS, scalar2=None,
                                                  op0=ALU.add), reads=[stB], writes=[stB])
            P.op("act", lambda e: e.activation(out=rs_ln[:, :], in_=rs_ln[:, :], func=AF.Sqrt), reads=[stB], writes=[stB])
            P.op("dve", lambda e: e.reciprocal(out=rs_ln[:, :], in_=rs_ln[:, :]), reads=[stB], writes=[stB])
            P.op("dve", lambda e: e.tensor_scalar(out=v1[:, :], in0=v2[:, :], scalar1=mv[:, 0:1], scalar2=rs_ln[:, 0:1],
                                                  op0=ALU.subtract, op1=ALU.mult), reads=[v2B, stB], writes=[v1B])
            P.op("dve", lambda e: e.tensor_tensor(out=v1[:, :], in0=v1[:, :], in1=lng[:, :], op=ALU.mult),
                 reads=[v1B, smallB], writes=[v1B])
            P.op("dve", lambda e: e.tensor_tensor(out=vn[:, :], in0=v1[:, :], in1=lnb[:, :], op=ALU.add),
                 reads=[v1B, smallB], writes=[vnB])
            if DBG_STOP < 8:
                continue
            for hb in range(2):
                pt, pb = P.next_psum()
                for j in range(3):
                    e_ = hb * 3 + j
                    tok = P.mm(pt[:, j * 128:(j + 1) * 128], [(vn[:, e_ * 128:(e_ + 1) * 128], wst[:, e_, :])],
                               reads=[vnB, wstB] + ([pb] if j > 0 else []), wbuf=Buf() if j > 0 else pb)
                    if j > 0:
                        pb.w = [tok]; pb.r = []
                for j in range(3 if DBG_STOP >= 9 else 0):
                    e_ = hb * 3 + j
                    P.op("dve", lambda e, pt=pt, j=j, e_=e_: e.tensor_tensor(
                        out=v2[:, 0:128], in0=pt[:, j * 128:(j + 1) * 128], in1=bs_t[:, e_ * 128:(e_ + 1) * 128],
                        op=ALU.add), reads=[pb, smallB], writes=[v2B])
                    P.op("act", lambda e, e_=e_, blk=blk: e.activation(
                        out=v1[:, 0:128], in_=yT[:, 10 + e_, blk * 128:(blk + 1) * 128], func=AF.Identity),
                        reads=[yB[10 + e_]], writes=[v1B])
                    P.op("dve", lambda e, e_=e_, blk=blk: e.tensor_tensor(
                        out=yT[:, 10 + e_, blk * 128:(blk + 1) * 128], in0=v2[:, 0:128],
                        in1=v1[:, 0:128], op=ALU.mult),
                        reads=[v2B, v1B, yB[10 + e_]], writes=[yB[10 + e_]])

        for k in range(16):
            P.dma("sp", ypart[:, k, :], yT[:, k, :], reads=[yB[k]], final=True)
        for k in range(6):
            P.dma("sp", zout[:, k, :], zT[:, k, :], reads=[zB[k]], final=True)
        P.emit()
    return nc


NLAG = 8192
CPC = 96


def build_B():
    nc = bass.Bass("TRN2", target_bir_lowering=False)

    def din(name, shape, dt=F32):
        return nc.dram_tensor(name, list(shape), dt, kind="ExternalInput").ap()

    zr = din("zr", [128, CPC, 188], BF16)
    featT = din("featT", [33, NLAG])
    tpos = din("tpos", [1, NLAG])
    ndelta = din("ndelta", [1, CPC])
    w1 = din("w1", [33, 64]); b1 = din("b1", [64, 1]); f1 = din("f1", [64, 1])
    w2 = din("w2", [64, 64]); b2 = din("b2", [64, 1]); f2 = din("f2", [64, 1])
    w3f = din("w3f", [64, CPC]); w3b = din("w3b", [64, CPC])
    ycv = nc.dram_tensor("ycv", [2, CPC, SEQ], F32, kind="ExternalOutput").ap()
    kfd_t = nc.dram_tensor("kfd", [CPC, NLAG], BF16)
    kfd = kfd_t.ap()

    with ExitStack() as es:
        P = Prog(nc, es)
        P.alloc_psum(8)
        T = P.tile
        zp = T("zp", [128, CPC, 188], BF16); zpB = Buf()
        ft = T("ft", [33, NLAG], F32); ftB = Buf()
        nd = T("nd", [1, CPC], F32)
        w1t = T("w1t", [33, 64], F32); w2t = T("w2t", [64, 64], F32)
        w3ft = T("w3ft", [64, CPC], F32); w3bt = T("w3bt", [64, CPC], F32)
        b1t = T("b1t", [64, 1], F32); f1t = T("f1t", [64, 1], F32); b2t = T("b2t", [64, 1], F32); f2t = T("f2t", [64, 1], F32)
        mpi = T("mpi", [128, 1], F32)
        smB = Buf()
        tlp = [(T(f"tlp{i}", [1, 512], F32), Buf()) for i in range(2)]
        arg = [(T(f"arg{i}", [64, 512], F32), Buf()) for i in range(2)]
        h1p = [(T(f"h1p{i}", [64, 512], F32), Buf()) for i in range(2)]
        h2p = [(T(f"h2p{i}", [64, 512], F32), Buf()) for i in range(2)]
        dec = [(T(f"dec{i}", [CPC, 512], F32), Buf()) for i in range(2)]
        kfs = T("kfs", [CPC, NLAG], BF16); kfB = Buf()
        NTC = 3
        tcs = [(T(f"tc{i}", [128, 63 * 128], BF16), Buf()) for i in range(NTC)]
        yo = [(T(f"yo{i}", [64, 8, 128], F32), Buf()) for i in range(2)]

        P.dma("sp", ft[:, :], featT, writes=[ftB])
        for (t_, d_) in ((nd, ndelta), (w1t, w1), (w2t, w2), (w3ft, w3f), (w3bt, w3b), (b1t, b1), (f1t, f1),
                         (b2t, b2), (f2t, f2)):
            P.dma("sp", t_[:, :], d_, writes=[smB], add_w=True)
        P.op("dve", lambda e: e.memset(mpi[:, :], -math.pi), writes=[smB], add_w=True)
        P.dma("sp", zp[:, :, :], zr, writes=[zpB])

        TWO_PI = 2.0 * math.pi
        ai = [0]

        ni_t = [(T(f"ni{i}", [64, 512], mybir.dt.int32), Buf()) for i in range(2)]
        nf_t = [(T(f"nf{i}", [64, 512], F32), Buf()) for i in range(2)]
        P.op("dve", lambda e: e.tensor_scalar(out=f1t[:, :], in0=f1t[:, :], scalar1=1.0 / TWO_PI, scalar2=None,
                                              op0=ALU.mult), reads=[smB], writes=[smB], add_w=True)
        P.op("dve", lambda e: e.tensor_scalar(out=f2t[:, :], in0=f2t[:, :], scalar1=1.0 / TWO_PI, scalar2=None,
                                              op0=ALU.mult), reads=[smB], writes=[smB], add_w=True)

        def sin_layer(pt, pb, bt, ft_, out_ap, outB):
            a, aB = arg[ai[0] % 2]
            ni, niB = ni_t[ai[0] % 2]
            nf, nfB = nf_t[ai[0] % 2]
            ai[0] += 1
            P.op("dve", lambda e: e.tensor_scalar(out=a[:, :], in0=pt[0:64, 0:512], scalar1=bt[:, 0:1],
                                                  scalar2=ft_[:, 0:1], op0=ALU.add, op1=ALU.mult),
                 reads=[pb, smB], writes=[aB])
            P.op("dve", lambda e: e.tensor_copy(out=ni[:, :], in_=a[:, :]), reads=[aB], writes=[niB])
            P.op("dve", lambda e: e.tensor_copy(out=nf[:, :], in_=ni[:, :]), reads=[niB], writes=[nfB])
            P.op("dve", lambda e: e.tensor_tensor(out=a[:, :], in0=a[:, :], in1=nf[:, :], op=ALU.subtract),
                 reads=[aB, nfB], writes=[aB])
            P.op("act", lambda e: e.activation(out=out_ap, in_=a[:, :], func=AF.Sin, scale=TWO_PI),
                 reads=[aB], writes=[outB])

        for i in range(NLAG // 512):
            c0 = i * 512
            pt, pb = P.next_psum()
            P.mm(pt[0:64, 0:512], [(w1t[:, :], ft[:, c0:c0 + 512])], reads=[smB, ftB], wbuf=pb)
            h1_, h1B = h1p[i % 2]
            sin_layer(pt, pb, b1t, f1t, h1_[:, :], h1B)
            pt, pb = P.next_psum()
            P.mm(pt[0:64, 0:512], [(w2t[:, :], h1_[:, :])], reads=[smB, h1B], wbuf=pb)
            h2_, h2B = h2p[i % 2]
            sin_layer(pt, pb, b2t, f2t, h2_[:, :], h2B)
            pt3, pb3 = P.next_psum()
            nb = max(0, min(c0 + 512, 4095) - c0)
            if nb > 0:
                P.mm(pt3[0:CPC, 0:nb], [(w3bt[:, :], h2_[:, 0:nb])], reads=[smB, h2B], wbuf=pb3)
            if nb < 512:
                if nb == 0:
                    P.mm(pt3[0:CPC, 0:512], [(w3ft[:, :], h2_[:, 0:512])], reads=[smB, h2B], wbuf=pb3)
                else:
                    tok = P.mm(pt3[0:CPC, nb:512], [(w3ft[:, :], h2_[:, nb:512])], reads=[smB, h2B, pb3], wbuf=Buf())
                    pb3.w = [tok]; pb3.r = []
            tl_, tlB = tlp[i % 2]
            P.dma("sp", tl_[:, :], tpos[:, c0:c0 + 512], writes=[tlB])
            ptd, pbd = P.next_psum()
            P.mm(ptd[0:CPC, 0:512], [(nd[:, :], tl_[:, :])], reads=[smB, tlB], wbuf=pbd)
            d_, dB = dec[i % 2]
            P.op("act", lambda e, d_=d_, ptd=ptd: e.activation(out=d_[:, :], in_=ptd[0:CPC, 0:512], func=AF.Exp),
                 reads=[pbd], writes=[dB])
            P.op("dve", lambda e, d_=d_, c0=c0, pt3=pt3: e.tensor_tensor(out=kfs[:, c0:c0 + 512], in0=pt3[0:CPC, 0:512],
                                                                       in1=d_[:, :], op=ALU.mult),
                 reads=[pb3, dB], writes=[kfB])
        kfdB = Buf()
        P.dma("sp", kfd, kfs[:, :], reads=[kfB], writes=[kfdB])

        for c in range(CPC):
            tc_, tcB = tcs[c % NTC]
            src = bass.AP(tensor=kfd_t, offset=c * NLAG, ap=[[1, 128], [1, 63 * 128]])
            P.dma("sp", tc_[:, :], src, reads=[kfdB], writes=[tcB])
            pt, pb = P.next_psum()

            def fn(pe, c=c, tc_=tc_, pt=pt):
                for di in range(63):
                    d = di - 31
                    ins = pe.matmul(pt[0:64, 0:128], zp[:, c, 2 * (31 - d):2 * (63 - d)], tc_[:, di * 128:(di + 1) * 128],
                                    start=(di == 0), stop=(di == 62))
                return ins
            P.op("pe", fn, reads=[zpB, tcB], writes=[pb])
            yo_, yoB = yo[(c // 8) % 2]
            P.op("act", lambda e, pt=pt, yo_=yo_, c=c: e.activation(out=yo_[:, c % 8, :], in_=pt[0:64, 0:128],
                                                                    func=AF.Identity), reads=[pb], writes=[yoB])
            if c % 8 == 7:
                c8 = c - 7
                for b in range(2):
                    dst = ycv[b, c8:c8 + 8, :].rearrange("c (i j) -> i c j", j=128)
                    P.dma("sp", dst, yo_[b:64:2, :, :], reads=[yoB], final=True)
        P.emit()
    return nc


TC_ = TOK + 2


def build_C():
    nc = bass.Bass("TRN2", target_bir_lowering=False)

    def din(name, shape, dt=F32):
        return nc.dram_tensor(name, list(shape), dt, kind="ExternalInput").ap()

    x1h = din("x1h", [128, 16, TC_])
    yph = din("yph", [128, 16, TC_], BF16)
    zh = din("zh", [128, 6, TC_], BF16)
    ych = din("ych", [128, 6, TC_])
    skip = din("skip", [128, 6])
    w_out = din("w_out", [D_MODEL, D_MODEL])
    g_ffn = din("g_ffn", [128, 16])
    w_up = din("w_up", [D_MODEL, 2 * D_FF])
    fcw = din("fcw", [128, 88, 3])
    fcb = din("fcb", [128, 88])
    w_down = din("w_down", [D_FF, D_MODEL])
    g_fin = din("g_fin", [128, 16])
    xo = nc.dram_tensor("xo", [128, 16, TOK], F32, kind="ExternalOutput").ap()
    xn = nc.dram_tensor("xn", [128, 16, TOK], F32, kind="ExternalOutput").ap()

    with ExitStack() as es:
        P = Prog(nc, es)
        P.alloc_psum(8)
        T = P.tile
        XT = T("XT", [128, 16, TC_], F32); xB = [Buf(f"x{k}") for k in range(16)]
        yT = T("yT", [128, 16, TC_], BF16); yB = [Buf(f"y{k}") for k in range(16)]
        h2T = yT; hB = yB
        NH = 4
        NQ = 44 // NH
        actT = T("actT", [128, NQ, TOK], BF16); aB = [Buf(f"a{k}") for k in range(NQ)]
        Fr = [(T(f"fr{i}", [128, TC_], F32), Buf()) for i in range(7)]
        sqs = [(T(f"sq{i}", [128, TC_], BF16), Buf()) for i in range(2)]
        ztmp = sqs
        yct = Fr[0:2]; tmpf = Fr[2:4]
        gbuf = Fr[0:2]; vbuf = Fr[2:4]
        cg, cgB = Fr[4]; cv, cvB = Fr[5]; sg, sgB = Fr[6]
        xnt = Fr[4:6]
        rstd = T("rstd", [128, TC_], F32); rstdB = Buf()
        ones = T("ones", [128, 128], BF16); onesB = Buf()
        sk = T("sk", [128, 6], F32); gf = T("gf", [128, 16], F32); gfin = T("gfin", [128, 16], F32)
        fcw_t = T("fcw", [128, 88, 3], F32); fcb_t = T("fcb", [128, 88], F32)
        smB = Buf()
        ws = WStream(P, "ws", 16, 128, 6)

        P.op("dve", lambda e: e.memset(ones[:, :], 1.0), writes=[onesB])
        for (t_, d_) in ((sk, skip), (gf, g_ffn), (gfin, g_fin), (fcb_t, fcb)):
            P.dma("sp", t_[:, :], d_, writes=[smB], add_w=True)
        P.dma("sp", fcw_t[:, :, :], fcw, writes=[smB], add_w=True)
        for k in range(16):
            P.dma("sp", XT[:, k, :], x1h[:, k, :], writes=[xB[k]])
            P.dma("sp", yT[:, k, :], yph[:, k, :], writes=[yB[k]])

        for e_ in range(6):
            zt, ztB = ztmp[e_ % 2]
            yc, ycB = yct[e_ % 2]
            tf, tfB = tmpf[e_ % 2]
            P.dma("sp", zt[:, :], zh[:, e_, :], writes=[ztB])
            P.dma("sp", yc[:, :], ych[:, e_, :], writes=[ycB])
            zf, zfB = Fr[4 + (e_ % 2)]
            P.op("act", lambda e, zt=zt, zf=zf: e.activation(out=zf[:, :], in_=zt[:, :], func=AF.Identity),
                 reads=[ztB], writes=[zfB])
            P.op("dve", lambda e, zf=zf, yc=yc, tf=tf, e_=e_: e.scalar_tensor_tensor(
                out=tf[:, :], in0=zf[:, :], scalar=sk[:, e_:e_ + 1], in1=yc[:, :], op0=ALU.mult, op1=ALU.add),
                reads=[zfB, ycB, smB], writes=[tfB])
            P.op("act", lambda e, zf=zf, e_=e_: e.activation(out=zf[:, :], in_=yT[:, 4 + e_, :], func=AF.Identity),
                 reads=[yB[4 + e_]], writes=[zfB])
            P.op("dve", lambda e, tf=tf, zf=zf, e_=e_: e.tensor_tensor(out=yT[:, 4 + e_, :], in0=tf[:, :], in1=zf[:, :],
                                                                      op=ALU.mult),
                 reads=[tfB, zfB, yB[4 + e_]], writes=[yB[4 + e_]])

        y_aps = [yT[:, k, :] for k in range(16)]
        pcs_c = pieces(TC_)
        for m in range(16):
            def consume(pi, s, w, pt, pb, m=m):
                P.op("dve", lambda e: e.tensor_tensor(out=XT[:, m, s:s + w], in0=pt[:, 0:w], in1=XT[:, m, s:s + w],
                                                      op=ALU.add), reads=[pb, xB[m]], writes=[xB[m]])
            proj(P, ws, w_out, 0, 16, m * 128, y_aps, yB, TC_, consume, pcs=pcs_c)

        def rmsnorm_rstd(ncols, col0):
            pcs = pieces(ncols)
            slots = [P.next_psum() for _ in pcs]
            for k in range(16):
                sq, sqb = sqs[k % 2]
                P.op("act", lambda e, k=k, sq=sq: e.activation(out=sq[:, 0:ncols], in_=XT[:, k, col0:col0 + ncols],
                                                               func=AF.Square), reads=[xB[k]], writes=[sqb])
                for (s, w), (pt, pb) in zip(pcs, slots):
                    def fn(pe, k=k, s=s, w=w, pt=pt, sq=sq):
                        return pe.matmul(pt[:, 0:w], ones[:, :], sq[:, s:s + w], start=(k == 0), stop=(k == 15))
                    if k == 0:
                        P.op("pe", fn, reads=[sqb, onesB], writes=[pb])
                    else:
                        tok = P.op("pe", fn, reads=[sqb, onesB, pb])
                        pb.w = [tok]; pb.r = []
            for (s, w), (pt, pb) in zip(pcs, slots):
                P.op("dve", lambda e, s=s, w=w, pt=pt: e.tensor_scalar(
                    out=rstd[:, s:s + w], in0=pt[:, 0:w], scalar1=1.0 / D_MODEL, scalar2=RMS_EPS,
                    op0=ALU.mult, op1=ALU.add), reads=[pb], writes=[rstdB])
            P.op("act", lambda e: e.activation(out=rstd[:, 0:ncols], in_=rstd[:, 0:ncols], func=AF.Sqrt),
                 reads=[rstdB], writes=[rstdB])
            P.op("dve", lambda e: e.reciprocal(out=rstd[:, 0:ncols], in_=rstd[:, 0:ncols]), reads=[rstdB], writes=[rstdB])

        rmsnorm_rstd(TC_, 0)
        for k in range(16):
            P.op("dve", lambda e, k=k: e.scalar_tensor_tensor(
                out=h2T[:, k, :], in0=XT[:, k, :], scalar=gf[:, k:k + 1], in1=rstd[:, :], op0=ALU.mult, op1=ALU.mult),
                reads=[xB[k], smB, rstdB], writes=[hB[k]])

        h_aps = [h2T[:, k, :] for k in range(16)]
        ev = [0]

        def evac_to(dst_t, dst_b):
            def consume(pi, s, w, pt, pb):
                P.op("act", lambda e: e.activation(out=dst_t[:, s:s + w], in_=pt[:, 0:w], func=AF.Identity),
                     reads=[pb], writes=[dst_b])
            return consume

        def conv3(src, srcB, ch, dst, dstB):
            P.op("dve", lambda e: e.tensor_scalar(
                out=dst[:, 0:TOK], in0=src[:, 1:1 + TOK], scalar1=fcw_t[:, ch, 1:2], scalar2=fcb_t[:, ch:ch + 1],
                op0=ALU.mult, op1=ALU.add), reads=[srcB, smB], writes=[dstB])
            P.op("dve", lambda e: e.scalar_tensor_tensor(
                out=dst[:, 0:TOK], in0=src[:, 0:TOK], scalar=fcw_t[:, ch, 0:1], in1=dst[:, 0:TOK],
                op0=ALU.mult, op1=ALU.add), reads=[srcB, dstB, smB], writes=[dstB])
            P.op("dve", lambda e: e.scalar_tensor_tensor(
                out=dst[:, 0:TOK], in0=src[:, 2:2 + TOK], scalar=fcw_t[:, ch, 2:3], in1=dst[:, 0:TOK],
                op0=ALU.mult, op1=ALU.add), reads=[srcB, dstB, smB], writes=[dstB])

        for hq in range(NH):
            for i in range(NQ):
                ff = hq * NQ + i
                gt, gB_ = gbuf[ev[0] % 2]
                vt_, vB_ = vbuf[ev[0] % 2]
                ev[0] += 1
                proj(P, ws, w_up, 0, 16, ff * 128, h_aps, hB, TC_, evac_to(gt, gB_), pcs=pcs_c)
                proj(P, ws, w_up, 0, 16, D_FF + ff * 128, h_aps, hB, TC_, evac_to(vt_, vB_), pcs=pcs_c)
                conv3(gt, gB_, ff, cg, cgB)
                conv3(vt_, vB_, 44 + ff, cv, cvB)
                P.op("act", lambda e: e.activation(out=sg[:, 0:TOK], in_=cg[:, 0:TOK], func=AF.Silu), reads=[cgB], writes=[sgB])
                P.op("dve", lambda e, i=i: e.tensor_tensor(out=actT[:, i, :], in0=sg[:, 0:TOK], in1=cv[:, 0:TOK], op=ALU.mult),
                     reads=[sgB, cvB], writes=[aB[i]])
            a_aps = [actT[:, i, :] for i in range(NQ)]
            for m in range(16):
                def consume(pi, s, w, pt, pb, m=m):
                    P.op("dve", lambda e: e.tensor_tensor(out=XT[:, m, 1 + s:1 + s + w], in0=pt[:, 0:w],
                                                          in1=XT[:, m, 1 + s:1 + s + w], op=ALU.add),
                         reads=[pb, xB[m]], writes=[xB[m]])
                proj(P, ws, w_down, hq * NQ * 128, NQ, m * 128, a_aps, aB, TOK, consume)

        for k in range(16):
            P.dma("sp", xo[:, k, :], XT[:, k, 1:1 + TOK], reads=[xB[k]], final=True)
        rmsnorm_rstd(TOK, 1)
        for k in range(16):
            xt_, xtB = xnt[k % 2]
            P.op("dve", lambda e, k=k, xt_=xt_: e.scalar_tensor_tensor(
                out=xt_[:, 0:TOK], in0=XT[:, k, 1:1 + TOK], scalar=gfin[:, k:k + 1], in1=rstd[:, 0:TOK],
                op0=ALU.mult, op1=ALU.mult), reads=[xB[k], smB, rstdB], writes=[xtB])
            P.dma("sp", xn[:, k, :], xt_[:, 0:TOK], reads=[xtB], final=True)
        P.emit()
    return nc


_CACHE = {}


def _prog(name):
    if name not in _CACHE:
        _CACHE[name] = {"A": build_A, "B": build_B, "C": build_C}[name]()
    return _CACHE[name]


def _fm(v, nchunk):
    return np.ascontiguousarray(np.asarray(v, np.float32).reshape(nchunk, 128).T)


def _filter_consts():
    L = SEQ
    t = np.linspace(0.0, 1.0, L, dtype=np.float32)[:, None]
    w = (np.float32(2.0 * math.pi) * np.arange(L, dtype=np.float32)[:, None] / np.float32(L)).astype(np.float32)
    f = np.linspace(1e-4, 16 - 1, 16, dtype=np.float32)[None, :]
    feat = np.concatenate([t, np.cos(f * w), -np.sin(f * w)], axis=-1).astype(np.float32)
    n = np.arange(NLAG)
    lag = np.abs(n - 4095)
    ok = lag < L
    lagc = np.minimum(lag, L - 1)
    featT = np.ascontiguousarray(feat[lagc].T * ok[None, :]).astype(np.float32)
    tl = (t[lagc, 0] * ok).astype(np.float32)
    deltas = np.linspace(math.log(1e-2) / 1.5, math.log(1e-2) / 0.3, D_HY, dtype=np.float32)
    return featT, tl, (-np.abs(deltas)).astype(np.float32)


def _rc_edge(q):
    out = np.zeros((4, 16), np.float32)
    t0 = q * TOK
    for g, w in enumerate((2, 4, 8, 16)):
        for j, t in enumerate(list(range(t0, t0 + 8)) + list(range(t0 + TOK - 8, t0 + TOK))):
            lo = max(t - w // 2, 0); hi = min(t + w // 2 - 1, SEQ - 1)
            out[g, j] = 1.0 / (hi - lo + 1)
    return np.ascontiguousarray(np.broadcast_to(out[None], (128, 4, 16)))


def _tok_shard(xT, halo):
    B, Fd, L = xT.shape
    pad = np.zeros((B, Fd, L + 2 * halo), xT.dtype)
    pad[:, :, halo:halo + L] = xT
    out = []
    for c in range(8):
        b, q = divmod(c, 4)
        sl = pad[b, :, q * TOK:q * TOK + TOK + 2 * halo]
        out.append(np.ascontiguousarray(sl.reshape(Fd // 128, 128, -1).transpose(1, 0, 2)))
    return out


def _untok(shards):
    nch = shards[0].shape[1]
    out = np.zeros((BATCH, nch * 128, SEQ), shards[0].dtype)
    for c in range(8):
        b, q = divmod(c, 4)
        out[b, :, q * TOK:(q + 1) * TOK] = shards[c].transpose(1, 0, 2).reshape(nch * 128, TOK)
    return out


def kernel(x, norm_mix, w_in, pool_w, pool_b, pool_scale, hy_short_w, hy_short_b,
           hy_filt_w1, hy_filt_b1, hy_filt_freq1, hy_filt_w2, hy_filt_b2, hy_filt_freq2,
           hy_filt_w3, hy_skip, gm_ln_g, gm_ln_b, gm_w_s, gm_b_s, w_out, norm_ffn,
           ffn_w_up, ffn_conv_w, ffn_conv_b, ffn_w_down, norm_final):
    f32 = lambda a: np.ascontiguousarray(np.asarray(a, dtype=np.float32))
    cores = list(range(8))
    xT = np.ascontiguousarray(f32(x).transpose(0, 2, 1))
    featT, tl, ndel = _filter_consts()
    pa, pb_, pc = _prog("A"), _prog("B"), _prog("C")
    xn_sh = None
    for l in range(DEPTH):
        xh = _tok_shard(xT, HALO)
        common = {
            "g_mix": _fm(norm_mix[l], 16), "w_in": f32(w_in[l]),
            "pool_w": f32(pool_w[l]).reshape(512, 128),
            "pool_b": _fm(f32(pool_b[l]).reshape(-1), 4), "pool_s": _fm(pool_scale[l], 4),
            "cw": np.ascontiguousarray(f32(hy_short_w[l]).reshape(3, 18, 128).transpose(2, 1, 0)),
            "cb": _fm(hy_short_b[l], 18),
            "ln_g": np.ascontiguousarray(np.broadcast_to(f32(gm_ln_g[l])[None], (128, 768))),
            "ln_b": np.ascontiguousarray(np.broadcast_to(f32(gm_ln_b[l])[None], (128, 768))),
            "wsT": np.ascontiguousarray(f32(gm_w_s[l]).transpose(0, 2, 1)).reshape(768, 128),
            "bsb": np.ascontiguousarray(np.broadcast_to(f32(gm_b_s[l]).reshape(1, 768), (128, 768))),
        }
        in_maps = [dict(common, xh=xh[c], rc_edge=_rc_edge(c % 4)) for c in cores]
        ra = run_bass_kernel_spmd(pa, in_maps, core_ids=cores).results
        ypart = _untok([np.asarray(ra[c]["ypart"]) for c in cores])
        zfull = _untok([np.asarray(ra[c]["zout"]) for c in cores])
        in_maps = []
        for j in cores:
            ch = slice(j * CPC, (j + 1) * CPC)
            zz = zfull[:, ch, :].reshape(2, CPC, 32, 128)[:, :, :, ::-1]
            zr = np.zeros((128, CPC, 94, 2), NPBF)
            zr[:, :, 31:63, :] = zz.transpose(3, 1, 2, 0)
            zr = zr.reshape(128, CPC, 188)
            in_maps.append({
                "zr": zr, "featT": featT,
                "tpos": np.ascontiguousarray(tl.reshape(1, NLAG)),
                "ndelta": np.ascontiguousarray(ndel[ch].reshape(1, CPC)),
                "w1": f32(hy_filt_w1[l]), "b1": f32(hy_filt_b1[l]).reshape(64, 1),
                "f1": f32(hy_filt_freq1[l]).reshape(64, 1),
                "w2": f32(hy_filt_w2[l]), "b2": f32(hy_filt_b2[l]).reshape(64, 1),
                "f2": f32(hy_filt_freq2[l]).reshape(64, 1),
                "w3f": f32(hy_filt_w3[l][:, j * CPC:(j + 1) * CPC]),
                "w3b": f32(hy_filt_w3[l][:, D_HY + j * CPC:D_HY + (j + 1) * CPC]),
            })
        rb = run_bass_kernel_spmd(pb_, in_maps, core_ids=cores).results
        ycv = np.concatenate([np.asarray(rb[j]["ycv"]) for j in cores], axis=1)
        x1h = _tok_shard(xT, 1)
        yph = _tok_shard(ypart, 1)
        zh = _tok_shard(zfull, 1)
        ych = _tok_shard(ycv, 1)
        common = {
            "skip": _fm(hy_skip[l], 6), "w_out": f32(w_out[l]), "g_ffn": _fm(norm_ffn[l], 16),
            "w_up": f32(ffn_w_up[l]),
            "fcw": np.ascontiguousarray(f32(ffn_conv_w[l]).reshape(3, 88, 128).transpose(2, 1, 0)),
            "fcb": _fm(ffn_conv_b[l], 88), "w_down": f32(ffn_w_down[l]), "g_fin": _fm(norm_final, 16),
        }
        in_maps = [dict(common, x1h=x1h[c], yph=yph[c], zh=zh[c], ych=ych[c]) for c in cores]
        rc = run_bass_kernel_spmd(pc, in_maps, core_ids=cores).results
        xT = _untok([np.asarray(rc[c]["xo"]) for c in cores])
        xn_sh = [np.asarray(rc[c]["xn"]) for c in cores]
    out = _untok(xn_sh)
    return np.ascontiguousarray(out.transpose(0, 2, 1)).astype(np.float32)
```
